# Optimizing a Trainium2 kernel written in Bass

```python
import jax, jax.numpy as jnp
from jax import lax
import numpy as np

D_MODEL = 2048
BATCH = 1
SEQ = 16384
DEPTH = 2

N_MEM = 256
EPS = 1e-6
NEG = -1e30
POOL_WINDOWS = (2, 4, 8, 16)
POOL_GROUPS = 4
POOL_GW = D_MODEL // 8
POOL_W = POOL_GROUPS * POOL_GW
SWA_CONFIGS = ((128, 1), (512, 4), (2048, 16))
SWA_GROUPS = 3
SWA_HPG = 4
SWA_HEADS = SWA_GROUPS * SWA_HPG
HEAD_DIM = 128
SWA_QKV = SWA_HEADS * HEAD_DIM
SWA_OUT = SWA_HPG * HEAD_DIM
BLK = 128
GLA_HEADS = 4
GLA_DK = D_MODEL // 16
GLA_DV = D_MODEL // 8
GLA_K = GLA_HEADS * GLA_DK
GLA_V = GLA_HEADS * GLA_DV
GLA_RANK = 16
GLA_TAU = 16.0
GLA_CHUNK = 64
MEM_HEADS = 4
MEM_W = MEM_HEADS * HEAD_DIM
N_BRANCH = 4
IN_SPLITS = (POOL_W, POOL_W,
             SWA_QKV, SWA_QKV, SWA_QKV, SWA_OUT,
             GLA_K, GLA_K, GLA_V, GLA_V, GLA_RANK,
             MEM_W, MEM_W,
             N_BRANCH * D_MODEL)
D_IN = sum(IN_SPLITS)

kernel_name = 'hybrid_pool_dilswa_gla_mem_block'


def rms_norm(x, g):
    xf = x.astype(jnp.float32)
    y = xf * lax.rsqrt(jnp.mean(xf * xf, axis=-1, keepdims=True) + EPS)
    return (y * g.astype(jnp.float32)).astype(x.dtype)


def pool_mixer(u, gate, w_pool, scale):
    B, S, _ = u.shape
    ug = u.reshape(B, S, POOL_GROUPS, POOL_GW).astype(jnp.float32)
    cs = jnp.cumsum(ug, axis=1)
    cs0 = jnp.pad(cs, ((0, 0), (1, 0), (0, 0), (0, 0)))
    pos = jnp.arange(S)
    outs = []
    for g, w in enumerate(POOL_WINDOWS):
        hi = cs[:, :, g]
        lo = jnp.pad(cs0[:, :S - w + 1, g], ((0, 0), (w - 1, 0), (0, 0)))
        cnt = jnp.minimum(pos + 1, w).astype(jnp.float32)[None, :, None]
        outs.append((hi - lo) / cnt - ug[:, :, g])
    pooled = jnp.stack(outs, axis=2).astype(u.dtype)
    mixed = jnp.einsum('bsgc,gcd->bsgd', pooled, w_pool).reshape(B, S, POOL_W) * scale
    return mixed * jax.nn.silu(gate)


def dilated_window_attn(q, k, v, window, dilation, slopes):
    B, S, H, Dh = q.shape
    nk = window // dilation
    L = -(-S // (dilation * BLK)) * BLK
    pad = L * dilation - S
    nb = L // BLK

    def to_blocks(t):
        t = jnp.pad(t, ((0, 0), (0, pad), (0, 0), (0, 0)))
        t = t.reshape(B, L, dilation, H, Dh).transpose(0, 2, 1, 3, 4)
        return t.reshape(B, dilation, nb, BLK, H, Dh)

    def band(t):
        prev = jnp.pad(t[:, :, :-1], ((0, 0), (0, 0), (1, 0), (0, 0), (0, 0), (0, 0)))
        return jnp.concatenate([prev, t], axis=3)

    qb = to_blocks(q)
    kk = band(to_blocks(k))
    vv = band(to_blocks(v))
    s = jnp.einsum('brnqhd,brnkhd->brnhqk', qb, kk).astype(jnp.float32) * (Dh ** -0.5)
    qi = jnp.arange(BLK)[:, None]
    kj = jnp.arange(2 * BLK)[None, :] - BLK
    delta = qi - kj
    first = (jnp.arange(nb) == 0)[:, None, None]
    valid = (delta >= 0) & (delta <= nk) & ~(first & (kj < 0))
    bias = -slopes[:, None, None] * (delta * dilation).astype(jnp.float32)
    s = jnp.where(valid[None, None, :, None], s + bias[None, None, None], NEG)
    m = jnp.max(s, axis=-1, keepdims=True)
    e = jnp.exp(s - m)
    den = jnp.sum(e, axis=-1, keepdims=True)
    o = jnp.einsum('brnhqk,brnkhd->brnqhd', (e / den).astype(v.dtype), vv)
    lse = (m + jnp.log(den))[..., 0]
    o = o.reshape(B, dilation, L, H, Dh).transpose(0, 2, 1, 3, 4).reshape(B, L * dilation, H, Dh)[:, :S]
    lse = lse.transpose(0, 1, 2, 4, 3).reshape(B, dilation, L, H).transpose(0, 2, 1, 3)
    lse = lse.reshape(B, L * dilation, H)[:, :S]
    return o, lse


def swa_mixer(q, k, v, gate):
    B, S, _ = q.shape
    slopes = jnp.exp2(-8.0 * (jnp.arange(SWA_HEADS, dtype=jnp.float32) + 1.0) / SWA_HEADS)
    qg = q.reshape(B, S, SWA_GROUPS, SWA_HPG, HEAD_DIM)
    kg = k.reshape(B, S, SWA_GROUPS, SWA_HPG, HEAD_DIM)
    vg = v.reshape(B, S, SWA_GROUPS, SWA_HPG, HEAD_DIM)
    outs, lses = [], []
    for g, (w, d) in enumerate(SWA_CONFIGS):
        o_g, l_g = dilated_window_attn(qg[:, :, g], kg[:, :, g], vg[:, :, g], w, d,
                                       slopes[g * SWA_HPG:(g + 1) * SWA_HPG])
        outs.append(o_g)
        lses.append(l_g)
    o = jnp.stack(outs, axis=2)
    wts = jax.nn.softmax(jnp.stack(lses, axis=2), axis=2)
    o = jnp.sum(wts[..., None].astype(o.dtype) * o, axis=2).reshape(B, S, SWA_OUT)
    return o * jax.nn.silu(gate)


def gla_mixer(q, k, v, lr, gate, w_alpha, b_alpha, g_gla):
    B, S, _ = q.shape
    C = GLA_CHUNK
    nc = S // C
    f32 = jnp.float32
    z = (lr @ w_alpha + b_alpha).astype(f32)
    log_a = jax.nn.log_sigmoid(z) / GLA_TAU
    qc = q.astype(f32).reshape(B, nc, C, GLA_HEADS, GLA_DK) * (GLA_DK ** -0.5)
    kc = k.astype(f32).reshape(B, nc, C, GLA_HEADS, GLA_DK)
    vc = v.astype(f32).reshape(B, nc, C, GLA_HEADS, GLA_DV)
    bc = jnp.cumsum(log_a.reshape(B, nc, C, GLA_HEADS, GLA_DK), axis=2)
    b_last = bc[:, :, -1]
    q_t = qc * jnp.exp(bc)
    k_t = kc * jnp.exp(-bc)
    causal = jnp.tril(jnp.ones((C, C), dtype=bool))
    a = jnp.where(causal, jnp.einsum('bcihk,bcjhk->bchij', q_t, k_t), 0.0)
    o_intra = jnp.einsum('bchij,bcjhv->bcihv', a, vc)
    kv = jnp.einsum('bcjhk,bcjhv->bchkv', kc * jnp.exp(b_last[:, :, None] - bc), vc)
    decay = jnp.exp(b_last)

    def step(state, inp):
        dec, kv_c = inp
        return dec[..., None] * state + kv_c, state

    _, s_prev = lax.scan(step, jnp.zeros((B, GLA_HEADS, GLA_DK, GLA_DV), f32),
                         (decay.transpose(1, 0, 2, 3), kv.transpose(1, 0, 2, 3, 4)))
    s_prev = s_prev.transpose(1, 0, 2, 3, 4)
    o = o_intra + jnp.einsum('bcihk,bchkv->bcihv', q_t, s_prev)
    o = o.reshape(B, S, GLA_HEADS, GLA_DV)
    o = o * lax.rsqrt(jnp.mean(o * o, axis=-1, keepdims=True) + EPS)
    o = o.reshape(B, S, GLA_V) * g_gla.astype(f32)
    return o.astype(q.dtype) * jax.nn.silu(gate)


def mem_attn(q, gate, mem, g_mem, w_mem_kv):
    B, S, _ = q.shape
    kv = rms_norm(mem, g_mem) @ w_mem_kv
    km, vm = jnp.split(kv, 2, axis=-1)
    km = km.reshape(B, N_MEM, MEM_HEADS, HEAD_DIM)
    vm = vm.reshape(B, N_MEM, MEM_HEADS, HEAD_DIM)
    qh = q.reshape(B, S, MEM_HEADS, HEAD_DIM)
    s = jnp.einsum('bshd,bmhd->bhsm', qh, km).astype(jnp.float32) * (HEAD_DIM ** -0.5)
    p = jax.nn.softmax(s, axis=-1)
    o = jnp.einsum('bhsm,bmhd->bshd', p.astype(vm.dtype), vm).reshape(B, S, MEM_W)
    return o * jax.nn.silu(gate)


def setup_inputs(seed: int = 0) -> dict:
    key = jax.random.key(seed)
    ks = jax.random.split(key, 20)
    f32 = jnp.float32
    nrm = lambda k, shape, scale: (jax.random.normal(k, shape, f32) * scale)
    return {
        'x': nrm(ks[0], (BATCH, SEQ, D_MODEL), 1.0),
        'mem': nrm(ks[1], (BATCH, N_MEM, D_MODEL), 1.0),
        'g_pre': 1.0 + nrm(ks[2], (DEPTH, D_MODEL), 0.05),
        'g_post': 1.0 + nrm(ks[3], (DEPTH, D_MODEL), 0.05),
        'g_mem': 1.0 + nrm(ks[4], (DEPTH, D_MODEL), 0.05),
        'w_in': nrm(ks[5], (DEPTH, D_MODEL, D_IN), D_MODEL ** -0.5),
        'b_merge': nrm(ks[6], (DEPTH, N_BRANCH, D_MODEL), 0.1),
        'w_pool': nrm(ks[7], (DEPTH, POOL_GROUPS, POOL_GW, POOL_GW), POOL_GW ** -0.5),
        'pool_scale': 1.0 + nrm(ks[8], (DEPTH, POOL_W), 0.05),
        'w_alpha': nrm(ks[9], (DEPTH, GLA_RANK, GLA_K), GLA_RANK ** -0.5),
        'b_alpha': nrm(ks[10], (DEPTH, GLA_K), 0.1),
        'g_gla': 1.0 + nrm(ks[11], (DEPTH, GLA_V), 0.05),
        'w_mem_kv': nrm(ks[12], (DEPTH, D_MODEL, 2 * MEM_W), D_MODEL ** -0.5),
        'w_br_pool': nrm(ks[13], (DEPTH, POOL_W, D_MODEL), POOL_W ** -0.5),
        'w_br_swa': nrm(ks[14], (DEPTH, SWA_OUT, D_MODEL), SWA_OUT ** -0.5),
        'w_br_gla': nrm(ks[15], (DEPTH, GLA_V, D_MODEL), GLA_V ** -0.5),
        'w_br_mem': nrm(ks[16], (DEPTH, MEM_W, D_MODEL), MEM_W ** -0.5),
        'w_out': nrm(ks[17], (DEPTH, D_MODEL, D_MODEL), D_MODEL ** -0.5),
    }


def reference(x, mem, g_pre, g_post, g_mem, w_in, b_merge, w_pool, pool_scale, w_alpha, b_alpha,
              g_gla, w_mem_kv, w_br_pool, w_br_swa, w_br_gla, w_br_mem, w_out):
    B, S, _ = x.shape
    split_at = np.cumsum(IN_SPLITS)[:-1]
    for l in range(DEPTH):
        h = rms_norm(x, g_pre[l])
        proj = h @ w_in[l]
        (a_val, a_gate, sq, sk, sv, s_gate, cq, ck, cv, c_gate, c_lr,
         mq, m_gate, g_logits) = jnp.split(proj, split_at, axis=-1)
        y_a = pool_mixer(a_val, a_gate, w_pool[l], pool_scale[l])
        y_b = swa_mixer(sq, sk, sv, s_gate)
        y_c = gla_mixer(cq, ck, cv, c_lr, c_gate, w_alpha[l], b_alpha[l], g_gla[l])
        y_m = mem_attn(mq, m_gate, mem, g_mem[l], w_mem_kv[l])
        gates = jax.nn.sigmoid(g_logits.reshape(B, S, N_BRANCH, D_MODEL) + b_merge[l])
        merged = (gates[:, :, 0] * (y_a @ w_br_pool[l]) + gates[:, :, 1] * (y_b @ w_br_swa[l])
                  + gates[:, :, 2] * (y_c @ w_br_gla[l]) + gates[:, :, 3] * (y_m @ w_br_mem[l]))
        x = x + rms_norm(merged @ w_out[l], g_post[l])
    return x
```

```python
import numpy as np
import concourse.bass as bass
import concourse.mybir as mybir

F32 = mybir.dt.float32
BF16 = mybir.dt.bfloat16
AF = mybir.ActivationFunctionType
ALU = mybir.AluOpType
AX = mybir.AxisListType


class Dep:
    __slots__ = ("name", "w", "r")

    def __init__(self, name=""):
        self.name = name
        self.w = None
        self.r = {}


class Ctr:
    def __init__(self, nc, name, step):
        self.sem = nc.alloc_semaphore(name)
        self.step = step
        self.cnt = 0
        self.name = name


class Eng:
    def __init__(self, fw, eng, name, is_dma_queue=False):
        self.fw = fw
        self.eng = eng
        self.name = name
        self.ctr = None if is_dma_queue else Ctr(fw.nc, "c_" + name, 1)
        self.seen = {}

    def _wait_for(self, deps, own=None):
        for ctr, c in deps.items():
            if ctr is own:
                continue
            if ctr.step == 16:
                c = ctr.cnt
            if ctr is self.ctr and (not self.fw.same_engine_waits or c > ctr.cnt):
                continue
            if self.seen.get(ctr, 0) < c:
                self.eng.wait_ge(ctr.sem, c * ctr.step)
                self.seen[ctr] = c

    def op(self, fn, reads=(), writes=(), ctr=None, inc=True):
        deps = {}

        def add(t):
            if t is None:
                return
            k, c = t
            if deps.get(k, 0) < c:
                deps[k] = c
        for d in reads:
            add(d.w)
        for d in writes:
            add(d.w)
            for k, c in d.r.items():
                add((k, c))
        ctr = ctr or self.ctr
        self._wait_for(deps, own=ctr if ctr.step == 16 else None)
        ins = fn()
        if inc:
            ctr.cnt += 1
            ins.then_inc(ctr.sem, ctr.step)
            tok = (ctr, ctr.cnt)
        else:
            tok = (ctr, ctr.cnt + 1)
        for d in reads:
            if d.r.get(tok[0], 0) < tok[1]:
                d.r[tok[0]] = tok[1]
        for d in writes:
            d.w = tok
            d.r = {}
        return ins


class FW:
    def __init__(self, nc, same_engine_waits=True):
        self.nc = nc
        self.same_engine_waits = same_engine_waits
        self.pe = Eng(self, nc.tensor, "pe")
        self.act = Eng(self, nc.scalar, "act")
        self.dve = Eng(self, nc.vector, "dve")
        self.pool = Eng(self, nc.gpsimd, "pool")
        self.sp = Eng(self, nc.sync, "sp", is_dma_queue=True)
        self.gq = Eng(self, nc.gpsimd, "gq", is_dma_queue=True)
        self.gq.seen = self.pool.seen
        self.all_dctrs = []
        self.free_dctrs = {"hw": [], "sw": []}

    def sbuf(self, name, shape, dtype):
        return self.nc.alloc_sbuf_tensor(name, list(shape), dtype)

    def psum(self, name, shape, dtype=F32):
        return self.nc.alloc_psum_tensor(name, list(shape), dtype)

    def dctr(self, kind="hw"):
        if self.free_dctrs[kind]:
            return self.free_dctrs[kind].pop()
        c = Ctr(self.nc, "d%s_%d" % (kind, len(self.all_dctrs)), 16)
        c.kind = kind
        self.all_dctrs.append(c)
        return c

    def finish(self, ctrs):
        for c in ctrs:
            if c.cnt:
                self.nc.sync.wait_ge(c.sem, c.cnt * c.step)
from concourse.bass_utils import run_bass_kernel_spmd

import contextlib


class Cfg:
    def __init__(self, P=128, KC=16):
        self.P = P; self.KC = KC; self.D = KC * P
        self.TPC = 16 * P; self.TT = 4 * P; self.NT = 4; self.NB = 16
        self.NC = 8; self.S = self.NC * self.TPC
        self.o_aval = 0; self.o_agate = 8 * P; self.o_sq = 16 * P; self.o_sk = 28 * P
        self.o_sv = 40 * P; self.o_sgate = 52 * P; self.o_cq = 56 * P; self.o_ck = 60 * P
        self.o_cv = 64 * P; self.o_cgate = 72 * P; self.o_lr = 80 * P
        self.o_mq = 80 * P + 16; self.o_mgate = 84 * P + 16; self.o_gl = 88 * P + 16
        self.DIN = 88 * P + 16 + 4 * self.D
        self.NY = 24
        self.EPS = 1e-6


class Rot:
    def __init__(self, K, name, shape, dtype, n, dma=True, psum=False, sw=False):
        self.items = []
        for i in range(n):
            t = K.psum(f"{name}{i}", shape, dtype) if psum else K.sbuf(f"{name}{i}", shape, dtype)
            self.items.append((t, Dep(f"{name}{i}"), K.dctr("sw" if sw else "hw") if dma else None))
        self.i = 0

    def next(self):
        it = self.items[self.i % len(self.items)]
        self.i += 1
        return it


class Kern:
    def __init__(self, nc, cfg, phase):
        self.nc = nc; self.cfg = cfg; self.phase = phase
        self.fw = FW(nc)
        self.stack = [contextlib.ExitStack()]
        self.uid = 0
        self.out_ctrs = []
        self.ev = 0
        self.ctr_scopes = [[]]

    def dctr(self, kind="hw"):
        c = self.fw.dctr(kind)
        self.ctr_scopes[-1].append(c)
        return c

    def sbuf(self, name, shape, dtype):
        self.uid += 1
        return self.stack[-1].enter_context(self.nc.sbuf_tensor(f"{name}_{self.uid}", list(shape), dtype))

    def psum(self, name, shape, dtype=F32):
        self.uid += 1
        return self.stack[-1].enter_context(self.nc.psum_tensor(f"{name}_{self.uid}", list(shape), dtype))

    @contextlib.contextmanager
    def scope(self):
        st = contextlib.ExitStack()
        self.stack.append(st)
        self.ctr_scopes.append([])
        try:
            yield
        finally:
            self.barrier()
            self.stack.pop()
            st.close()
            for c in self.ctr_scopes.pop():
                self.fw.free_dctrs[c.kind].append(c)

    def barrier(self):
        fw = self.fw
        ctrs = [e.ctr for e in (fw.pe, fw.act, fw.dve, fw.pool)] + self.fw.all_dctrs
        for e in (fw.pe, fw.act, fw.dve, fw.pool, fw.sp):
            for c in ctrs:
                if c.cnt and e.seen.get(c, 0) < c.cnt:
                    e.eng.wait_ge(c.sem, c.cnt * c.step)
                    e.seen[c] = c.cnt

    def dma(self, out, in_, reads=(), writes=(), ctr=None, cast=False):
        q = self.fw.gq if cast else self.fw.sp
        eng = self.nc.gpsimd if cast else self.nc.sync
        return q.op(lambda: eng.dma_start(out=out, in_=in_), reads=reads, writes=writes, ctr=ctr)

    def const(self, name, shape, dtype, src, cast=False):
        t = self.sbuf(name, shape, dtype); d = Dep(name); c = self.dctr("sw" if cast else "hw")
        self.dma(t[:], src, writes=[d], ctr=c, cast=cast)
        return t, d

    def evac(self, out, in_, reads, writes):
        self.ev += 1
        if self.ev % 2:
            return self.fw.act.op(lambda: self.nc.scalar.copy(out=out, in_=in_), reads=reads, writes=writes)
        return self.fw.dve.op(lambda: self.nc.vector.tensor_copy(out=out, in_=in_), reads=reads, writes=writes)

    def loadw(self, pool, src, n):
        t, d, c = pool.next()
        kc = src.shape[0] // self.cfg.P
        self.dma(t[:, 0:kc, 0:n], src.rearrange("(c p) n -> p c n", p=self.cfg.P), writes=[d], ctr=c, cast=True)
        return t, d

    def mm_chain(self, ps, pd, M, N, lhs_fn, rhs_fn, kc, reads):
        nc = self.nc
        for c in range(kc):
            self.fw.pe.op(lambda: nc.tensor.matmul(ps[0:M, 0:N], lhsT=lhs_fn(c), rhs=rhs_fn(c),
                                                    start=(c == 0), stop=(c == kc - 1)),
                          reads=reads, writes=[pd], inc=(c == kc - 1))

    def proj_fm(self, wt, wd, c0, M, src, sd, tok0, N):
        ps, pd, _ = self.mmp.next()
        self.mm_chain(ps, pd, M, N, lambda c: wt[:, c, c0:c0 + M], lambda c: src[:, c, tok0:tok0 + N],
                      self.cfg.KC, [wd, sd])
        return ps, pd

    def norm_T(self, x_ap, nblk, g_t, g_d, dst, dst_d):
        cfg, nc, fw = self.cfg, self.nc, self.fw
        P, D, KC = cfg.P, cfg.D, cfg.KC
        with self.scope():
            xp = Rot(self, "xin", [P, D], F32, 2)
            junk = self.sbuf("junk", [P, D], BF16); dj = Dep()
            xn = Rot(self, "xn", [P, D], BF16, 2, dma=False)
            ssr = Rot(self, "ss", [P, 2], F32, 2, dma=False)
            for tb in range(nblk):
                xt, dx, cx = xp.next()
                self.dma(xt[:], x_ap[tb * P:(tb + 1) * P, :], writes=[dx], ctr=cx)
                ss, dss, _ = ssr.next()
                fw.act.op(lambda: nc.scalar.activation(out=junk[:], in_=xt[:], func=AF.Square, accum_out=ss[:, 0:1]),
                          reads=[dx], writes=[dj, dss])
                fw.act.op(lambda: nc.scalar.activation(out=ss[:, 1:2], in_=ss[:, 0:1], func=AF.Ln, scale=1.0 / D, bias=cfg.EPS),
                          reads=[dss], writes=[dss])
                fw.act.op(lambda: nc.scalar.activation(out=ss[:, 0:1], in_=ss[:, 1:2], func=AF.Exp, scale=-0.5),
                          reads=[dss], writes=[dss])
                xb, dxb, _ = xn.next()
                fw.dve.op(lambda: nc.vector.tensor_scalar(out=xb[:], in0=xt[:], scalar1=ss[:, 0:1], scalar2=None, op0=ALU.mult),
                          reads=[dx, dss], writes=[dxb])
                for c in range(KC):
                    pt, dpt, _ = self.trp.next()
                    fw.pe.op(lambda: nc.tensor.transpose(out=pt, in_=xb[:, c * P:(c + 1) * P], identity=self.ident[:]),
                             reads=[dxb, self.d_ident], writes=[dpt])
                    o = dst[:, c, tb * P:(tb + 1) * P]
                    if c % 2:
                        fw.act.op(lambda: nc.scalar.mul(out=o, in_=pt, mul=g_t[:, c:c + 1]), reads=[dpt, g_d], writes=[dst_d])
                    else:
                        fw.dve.op(lambda: nc.vector.tensor_scalar(out=o, in0=pt, scalar1=g_t[:, c:c + 1], scalar2=None, op0=ALU.mult),
                                  reads=[dpt, g_d], writes=[dst_d])

    def setup(self, io):
        cfg, nc = self.cfg, self.nc
        P, KC, TPC = cfg.P, cfg.KC, cfg.TPC
        self.io = io
        self.mmp = Rot(self, "mmps", [P, 512], F32, 2, dma=False, psum=True)
        self.auxp = Rot(self, "auxps", [P, 512], F32, 4, dma=False, psum=True)
        self.trp = Rot.__new__(Rot); self.trp.i = 0; self.trp.items = []
        for i in range(2):
            trt = self.psum("trps%d" % i, [P, 1024], BF16)
            self.trp.items.append((trt[:, 0:P], Dep("tr%d" % i), None))
        self.ident, self.d_ident = self.const("ident", [P, P], BF16, io["c_ident"][:, :], cast=True)
        self.ones, self.d_ones = self.const("ones", [P, P], BF16, io["c_ones"][:, :], cast=True)
        self.gpre, self.d_gpre = self.const("gpre", [P, KC], F32, io["g_pre"][:, :])
        self.hT = self.sbuf("hT", [P, KC, TPC], BF16); self.d_hT = Dep("hT")
        self.norm_T(io["x"], cfg.NB, self.gpre, self.d_gpre, self.hT, self.d_hT)

    def mkw(self, kinds):
        P, KC = self.cfg.P, self.cfg.KC
        if "P" in kinds:
            self.wP = Rot(self, "wP", [P, KC, P], BF16, 3, sw=True)
        if "2" in kinds:
            self.w2P = Rot(self, "w2P", [P, KC, 2 * P], BF16, 2, sw=True)
        if "4" in kinds:
            self.w4P = Rot(self, "w4P", [P, KC, 4 * P], BF16, 2, sw=True)

    def win(self, c0, n):
        return self.io["w_in"][:, c0:c0 + n]

    def gla(self, out):
        cfg, nc, fw, io = self.cfg, self.nc, self.fw, self.io
        P, KC, TPC, TT, NT, NB = cfg.P, cfg.KC, cfg.TPC, cfg.TT, cfg.NT, cfg.NB
        hT, dh = self.hT, self.d_hT
        with self.scope():
            self.mkw("P2")
            U, dU = self.const("cU", [P, P], F32, io["c_U"][:, :])
            Mrev, dMrev = self.const("cMrev", [P, P], F32, io["c_Mrev"][:, :])
            walpha, dwa = self.const("walpha", [32, 4 * P], F32, io["w_alpha_aug"][:, :])
            lr = self.sbuf("lr_aug", [32, TPC], F32); dlr = Dep("lr")
            fw.dve.op(lambda: nc.vector.memset(lr[:], 1.0), writes=[dlr])
            w16p = Rot(self, "w16", [P, KC, 16], BF16, 1, sw=True)
            wl, dwl = self.loadw(w16p, self.win(cfg.o_lr, 16), 16)
            for t in range(NT):
                ps, pd = self.proj_fm(wl, dwl, 0, 16, hT, dh, t * TT, TT)
                self.evac(lr[0:16, t * TT:(t + 1) * TT], ps[0:16, 0:TT], [pd], [dlr])
            ktm = self.sbuf("ktm", [P, NB, P], BF16); dktm = Dep()
            vtm = self.sbuf("vtm", [P, NB, 2 * P], BF16); dvtm = Dep()
            S = self.sbuf("S", [P, 2 * P], F32); dS = Dep()
            Sb = self.sbuf("Sb", [P, 2 * P], BF16); dSb = Dep()
            f32r = Rot(self, "gf", [P, P], F32, 6, dma=False)
            bfr = Rot(self, "gb", [P, P], BF16, 6, dma=False)
            decr = Rot(self, "dec", [P, 1], F32, 2, dma=False)
            if out:
                Mc, dMc = self.const("cMc", [P, P], F32, io["c_Mc"][:, :])
                ggla, dgg = self.const("ggla", [P, 8], F32, io["g_gla"][:, :])
                QT = self.sbuf("gQT", [P, TPC], BF16); dQT = Dep()
                KT = self.sbuf("gKT", [P, TPC], BF16); dKT = Dep()
                sg = self.sbuf("gsg", [P, 2, TPC], BF16); dsg = Dep()
                yb = Rot(self, "gy", [P, TPC], BF16, 2)
                osb = Rot(self, "gosb", [P, P], F32, 4, dma=False)
                Sent, dSent = self.s_enter()
            else:
                dtot = self.sbuf("dtot", [P, 4], F32); ddt = Dep()
                fw.dve.op(lambda: nc.vector.memset(dtot[:], 1.0), writes=[ddt])
                c_so = self.dctr(); self.out_ctrs.append(c_so)
            for h in range(4):
                wk, dwk = self.loadw(self.wP, self.win(cfg.o_ck + h * P, P), P)
                wv, dwv = self.loadw(self.w2P, self.win(cfg.o_cv + 2 * h * P, 2 * P), 2 * P)
                for tb in range(NB):
                    ps, pd, _ = self.auxp.next()
                    self.mm_chain(ps, pd, P, P, lambda c: hT[:, c, tb * P:(tb + 1) * P], lambda c: wk[:, c, 0:P], KC, [dh, dwk])
                    self.evac(ktm[:, tb, :], ps[0:P, 0:P], [pd], [dktm])
                    ps, pd, _ = self.auxp.next()
                    self.mm_chain(ps, pd, P, 2 * P, lambda c: hT[:, c, tb * P:(tb + 1) * P], lambda c: wv[:, c, 0:2 * P], KC, [dh, dwv])
                    self.evac(vtm[:, tb, :], ps[0:P, 0:2 * P], [pd], [dvtm])
                if out:
                    wq, dwq = self.loadw(self.wP, self.win(cfg.o_cq + h * P, P), P)
                    wg, dwg = self.loadw(self.w2P, self.win(cfg.o_cgate + 2 * h * P, 2 * P), 2 * P)
                    for t in range(NT):
                        ps, pd = self.proj_fm(wq, dwq, 0, P, hT, dh, t * TT, TT)
                        self.evac(QT[:, t * TT:(t + 1) * TT], ps[:, 0:TT], [pd], [dQT])
                        ps, pd = self.proj_fm(wk, dwk, 0, P, hT, dh, t * TT, TT)
                        self.evac(KT[:, t * TT:(t + 1) * TT], ps[:, 0:TT], [pd], [dKT])
                        for c in range(2):
                            ps, pd = self.proj_fm(wg, dwg, c * P, P, hT, dh, t * TT, TT)
                            fw.act.op(lambda: nc.scalar.activation(out=sg[:, c, t * TT:(t + 1) * TT], in_=ps[:, 0:TT], func=AF.Silu),
                                      reads=[pd], writes=[dsg])
                    fw.dve.op(lambda: nc.vector.tensor_copy(out=S[:], in_=Sent[:, h, :]), reads=[dSent], writes=[dS])
                    ybufs = [yb.next(), yb.next()]
                else:
                    fw.dve.op(lambda: nc.vector.memset(S[:], 0.0), writes=[dS])
                fw.act.op(lambda: nc.scalar.copy(out=Sb[:], in_=S[:]), reads=[dS], writes=[dSb])
                for tb in range(NB):
                    blk = slice(tb * P, (tb + 1) * P)
                    zp, dzp, _ = self.auxp.next()
                    fw.pe.op(lambda: nc.tensor.matmul(zp[:, 0:P], lhsT=lr[0:32, blk], rhs=walpha[0:32, h * P:(h + 1) * P], start=True, stop=True),
                             reads=[dlr, dwa], writes=[dzp])
                    e1, de1, _ = f32r.next()
                    fw.act.op(lambda: nc.scalar.activation(out=e1[:], in_=zp[:, 0:P], func=AF.Exp, scale=-1.0), reads=[dzp], writes=[de1])
                    la, dla, _ = f32r.next()
                    fw.act.op(lambda: nc.scalar.activation(out=la[:], in_=e1[:], func=AF.Ln, bias=1.0), reads=[de1], writes=[dla])
                    rp, drp, _ = self.auxp.next()
                    fw.pe.op(lambda: nc.tensor.matmul(rp[:, 0:P], lhsT=Mrev[:], rhs=la[:], start=True, stop=True),
                             reads=[dMrev, dla], writes=[drp])
                    er, der, _ = f32r.next()
                    fw.act.op(lambda: nc.scalar.activation(out=er[:], in_=rp[:, 0:P], func=AF.Exp), reads=[drp], writes=[der])
                    kdec, dkd, _ = bfr.next()
                    fw.dve.op(lambda: nc.vector.tensor_tensor(out=kdec[:], in0=ktm[:, tb, :], in1=er[:], op=ALU.mult),
                              reads=[dktm, der], writes=[dkd])
                    if out:
                        bp, dbp, _ = self.auxp.next()
                        fw.pe.op(lambda: nc.tensor.matmul(bp[:, 0:P], lhsT=la[:], rhs=U[:], start=True, stop=True),
                                 reads=[dla, dU], writes=[dbp])
                        Ep, dEp, _ = f32r.next()
                        fw.act.op(lambda: nc.scalar.activation(out=Ep[:], in_=bp[:, 0:P], func=AF.Exp), reads=[dbp], writes=[dEp])
                        Em, dEm, _ = f32r.next()
                        fw.act.op(lambda: nc.scalar.activation(out=Em[:], in_=bp[:, 0:P], func=AF.Exp, scale=-1.0), reads=[dbp], writes=[dEm])
                        qt, dqt, _ = bfr.next()
                        fw.dve.op(lambda: nc.vector.scalar_tensor_tensor(out=qt[:], in0=QT[:, blk], scalar=float(P) ** -0.5, in1=Ep[:],
                                                                         op0=ALU.mult, op1=ALU.mult), reads=[dQT, dEp], writes=[dqt])
                        kt, dkt, _ = bfr.next()
                        fw.dve.op(lambda: nc.vector.tensor_tensor(out=kt[:], in0=KT[:, blk], in1=Em[:], op=ALU.mult),
                                  reads=[dKT, dEm], writes=[dkt])
                        dec_ap, ddec = Ep[:, P - 1:P], dEp
                        ap_, dap, _ = self.auxp.next()
                        fw.pe.op(lambda: nc.tensor.matmul(ap_[:, 0:P], lhsT=kt[:], rhs=qt[:], start=True, stop=True),
                                 reads=[dkt, dqt], writes=[dap])
                        am, dam, _ = bfr.next()
                        fw.dve.op(lambda: nc.vector.tensor_tensor(out=am[:], in0=ap_[:, 0:P], in1=Mc[:], op=ALU.mult),
                                  reads=[dap, dMc], writes=[dam])
                        osq = []; osbs = []
                        for c in range(2):
                            op_, dop, _ = self.auxp.next()
                            fw.pe.op(lambda: nc.tensor.matmul(op_[:, 0:P], lhsT=vtm[:, tb, c * P:(c + 1) * P], rhs=am[:], start=True, stop=False),
                                     reads=[dvtm, dam], writes=[dop], inc=False)
                            fw.pe.op(lambda: nc.tensor.matmul(op_[:, 0:P], lhsT=Sb[:, c * P:(c + 1) * P], rhs=qt[:], start=False, stop=True),
                                     reads=[dSb, dqt], writes=[dop])
                            ob, dob, _ = osb.next()
                            fw.act.op(lambda: nc.scalar.copy(out=ob[:], in_=op_[:, 0:P]), reads=[dop], writes=[dob])
                            sq, dsq, _ = bfr.next()
                            fw.dve.op(lambda: nc.vector.tensor_tensor(out=sq[:], in0=ob[:], in1=ob[:], op=ALU.mult), reads=[dob], writes=[dsq])
                            osq.append((sq, dsq)); osbs.append((ob, dob))
                        sp_, dsp, _ = self.auxp.next()
                        for c in range(2):
                            fw.pe.op(lambda: nc.tensor.matmul(sp_[:, 0:P], lhsT=self.ones[:], rhs=osq[c][0][:], start=(c == 0), stop=(c == 1)),
                                     reads=[self.d_ones, osq[c][1]], writes=[dsp], inc=(c == 1))
                        rs, drs, _ = f32r.next()
                        fw.act.op(lambda: nc.scalar.activation(out=rs[:], in_=sp_[:, 0:P], func=AF.Ln, scale=1.0 / (2 * P), bias=cfg.EPS),
                                  reads=[dsp], writes=[drs])
                        fw.act.op(lambda: nc.scalar.activation(out=rs[:], in_=rs[:], func=AF.Exp, scale=-0.5), reads=[drs], writes=[drs])
                        for c in range(2):
                            ob, dob = osbs[c]
                            fw.dve.op(lambda: nc.vector.scalar_tensor_tensor(out=ob[:], in0=ob[:], scalar=ggla[:, 2 * h + c:2 * h + c + 1], in1=rs[:],
                                                                             op0=ALU.mult, op1=ALU.mult), reads=[dob, dgg, drs], writes=[dob])
                            fw.dve.op(lambda: nc.vector.tensor_tensor(out=ybufs[c][0][:, blk], in0=ob[:], in1=sg[:, c, blk], op=ALU.mult),
                                      reads=[dob, dsg], writes=[ybufs[c][1]])
                    else:
                        dp_, ddp, _ = self.auxp.next()
                        fw.pe.op(lambda: nc.tensor.matmul(dp_[:, 0:1], lhsT=la[:], rhs=U[:, P - 1:P], start=True, stop=True),
                                 reads=[dla, dU], writes=[ddp])
                        dc, ddc, _ = decr.next()
                        fw.act.op(lambda: nc.scalar.activation(out=dc[:], in_=dp_[:, 0:1], func=AF.Exp), reads=[ddp], writes=[ddc])
                        dec_ap, ddec = dc[:, 0:1], ddc
                        fw.dve.op(lambda: nc.vector.tensor_tensor(out=dtot[:, h:h + 1], in0=dtot[:, h:h + 1], in1=dc[:, 0:1], op=ALU.mult),
                                  reads=[ddc, ddt], writes=[ddt])
                    kp, dkp, _ = self.auxp.next()
                    fw.pe.op(lambda: nc.tensor.matmul(kp[:, 0:2 * P], lhsT=kdec[:], rhs=vtm[:, tb, :], start=True, stop=True),
                             reads=[dkd, dvtm], writes=[dkp])
                    fw.dve.op(lambda: nc.vector.scalar_tensor_tensor(out=S[:], in0=S[:], scalar=dec_ap, in1=kp[:, 0:2 * P],
                                                                     op0=ALU.mult, op1=ALU.add), reads=[dS, ddec, dkp], writes=[dS])
                    fw.act.op(lambda: nc.scalar.copy(out=Sb[:], in_=S[:]), reads=[dS], writes=[dSb])
                if out:
                    for c in range(2):
                        j = 12 + 2 * h + c
                        self.dma(self.ysc[j, :, :], ybufs[c][0][:], reads=[ybufs[c][1]], writes=[self.d_ysc[j]], ctr=ybufs[c][2])
                else:
                    self.dma(io["o_glaS"][:, h, :], S[:], reads=[dS], ctr=c_so)
            if not out:
                self.dma(io["o_glaD"][:, :], dtot[:], reads=[ddt], ctr=c_so)

    def s_enter(self):
        cfg, nc, fw, io = self.cfg, self.nc, self.fw, self.io
        P = cfg.P
        sel, dsel = self.const("sel", [P, 8], F32, io["sel"][:, :])
        T = self.sbuf("seT", [P, 4, 2 * P], F32); dT = Dep()
        acc = self.sbuf("seA", [P, 4, 2 * P], F32); dacc = Dep()
        fw.dve.op(lambda: nc.vector.memset(T[:], 0.0), writes=[dT])
        fw.dve.op(lambda: nc.vector.memset(acc[:], 0.0), writes=[dacc])
        sj = Rot(self, "seS", [P, 4, 2 * P], F32, 2)
        dj = Rot(self, "seD", [P, 4], F32, 2)
        for j in range(cfg.NC):
            fw.dve.op(lambda: nc.vector.scalar_tensor_tensor(out=acc[:], in0=T[:], scalar=sel[:, j:j + 1], in1=acc[:],
                                                             op0=ALU.mult, op1=ALU.add), reads=[dT, dsel, dacc], writes=[dacc])
            if j == cfg.NC - 1:
                break
            s_, ds_, cs_ = sj.next(); d_, dd_, cd_ = dj.next()
            self.dma(s_[:], io["S_all"][j, :, :, :], writes=[ds_], ctr=cs_)
            self.dma(d_[:], io["D_all"][j, :, :], writes=[dd_], ctr=cd_)
            for h in range(4):
                fw.dve.op(lambda: nc.vector.scalar_tensor_tensor(out=T[:, h, :], in0=T[:, h, :], scalar=d_[:, h:h + 1], in1=s_[:, h, :],
                                                                 op0=ALU.mult, op1=ALU.add), reads=[dT, dd_, ds_], writes=[dT])
        return acc, dacc

    def phaseA(self):
        cfg, nc, fw, io = self.cfg, self.nc, self.fw, self.io
        P, KC, TPC, TT, NT = cfg.P, cfg.KC, cfg.TPC, cfg.TT, cfg.NT
        hT, dh = self.hT, self.d_hT
        with self.scope():
            self.mkw("P4")
            ktb = Rot(self, "ktb", [P, TPC], BF16, 2)
            for h in range(12):
                w, dw = self.loadw(self.wP, self.win(cfg.o_sk + h * P, P), P)
                kb, dkb, ckb = ktb.next()
                for t in range(NT):
                    ps, pd = self.proj_fm(w, dw, 0, P, hT, dh, t * TT, TT)
                    self.evac(kb[:, t * TT:(t + 1) * TT], ps[:, 0:TT], [pd], [dkb])
                self.dma(io["o_kt"][h, :, :], kb[:], reads=[dkb], ctr=ckb)
                if ckb not in self.out_ctrs:
                    self.out_ctrs.append(ckb)
            vb = Rot(self, "vb", [P, 4 * P], BF16, 2)
            for g in range(3):
                dil = 4 ** g
                w, dw = self.loadw(self.w4P, self.win(cfg.o_sv + g * 4 * P, 4 * P), 4 * P)
                for s in range(16):
                    n, r = s // dil, s % dil
                    ks = slice(n * dil * P + r, (n + 1) * dil * P, dil)
                    ps, pd, _ = self.mmp.next()
                    self.mm_chain(ps, pd, P, 4 * P, lambda c: hT[:, c, ks], lambda c: w[:, c, 0:4 * P], KC, [dh, dw])
                    v_, dv_, cv_ = vb.next()
                    self.evac(v_[:], ps[:, 0:4 * P], [pd], [dv_])
                    self.dma(io["o_v"][g, s, :, :], v_[:], reads=[dv_], ctr=cv_)
                    if cv_ not in self.out_ctrs:
                        self.out_ctrs.append(cv_)
            tail = self.sbuf("tail", [P, 8, 16], F32); dtl = Dep()
            for g2 in range(2):
                w, dw = self.loadw(self.w4P, self.win(cfg.o_aval + g2 * 4 * P, 4 * P), 4 * P)
                for j in range(4):
                    ps, pd = self.proj_fm(w, dw, j * P, P, hT, dh, TPC - 16, 16)
                    self.evac(tail[:, g2 * 4 + j, :], ps[:, 0:16], [pd], [dtl])
            ct = self.dctr(); self.out_ctrs.append(ct)
            self.dma(io["o_tail"][:, :, :], tail[:], reads=[dtl], ctr=ct)
        self.gla(out=False)

    def phaseB(self):
        cfg, nc, fw, io = self.cfg, self.nc, self.fw, self.io
        P, KC, TPC, TT, NT, D = cfg.P, cfg.KC, cfg.TPC, cfg.TT, cfg.NT, cfg.D
        hT, dh = self.hT, self.d_hT
        self.ysc = io["ysc"]
        self.d_ysc = [Dep(f"ysc{j}") for j in range(cfg.NY)]
        self.pool_mixer()
        self.mem_attn()
        self.swa()
        self.gla(out=True)
        self.merge_out()

    def pool_mixer(self):
        cfg, nc, fw, io = self.cfg, self.nc, self.fw, self.io
        P, KC, TPC, TT, NT = cfg.P, cfg.KC, cfg.TPC, cfg.TT, cfg.NT
        hT, dh = self.hT, self.d_hT
        with self.scope():
            self.mkw("2")
            wpool, dwp = self.const("wpool", [P, 8, 2 * P], BF16, io["w_pool"].rearrange("(gc p) n -> p gc n", p=P), cast=True)
            pscale, dpsc = self.const("pscale", [P, 8], F32, io["pool_scale"][:, :])
            invc, dinv = self.const("invc", [P, 4, 16], F32, io["invcnt"][:, :, :])
            uext = Rot(self, "uext", [P, 16 + TPC], F32, 2)
            wsA = self.sbuf("wsA", [P, 16 + TPC], F32); dA = Dep()
            wsB = self.sbuf("wsB", [P, 16 + TPC], F32); dB = Dep()
            pooled = [(self.sbuf(f"pooled{c}", [P, TPC], BF16), Dep()) for c in range(2)]
            sgs = [(self.sbuf(f"psg{c}", [P, TPC], BF16), Dep()) for c in range(2)]
            t16 = self.sbuf("t16", [P, 16], F32); dt16 = Dep()
            yb = Rot(self, "py", [P, TPC], BF16, 2)
            for g in range(4):
                w = (2, 4, 8, 16)[g]
                wv, dwv = self.loadw(self.w2P, self.win(cfg.o_aval + g * 2 * P, 2 * P), 2 * P)
                wg, dwg = self.loadw(self.w2P, self.win(cfg.o_agate + g * 2 * P, 2 * P), 2 * P)
                for c in range(2):
                    j = 2 * g + c
                    u, du, cu = uext.next()
                    self.dma(u[:, 0:16], io["pool_halo"][:, j, :], writes=[du], ctr=cu)
                    for t in range(NT):
                        ps, pd = self.proj_fm(wv, dwv, c * P, P, hT, dh, t * TT, TT)
                        self.evac(u[:, 16 + t * TT:16 + (t + 1) * TT], ps[:, 0:TT], [pd], [du])
                    src, dsrc = u, du
                    step, lo = 1, 0
                    bufs = [(wsA, dA), (wsB, dB)]
                    k = 0
                    while step < w:
                        dst, ddst = bufs[k % 2]; k += 1
                        lo += step
                        fw.dve.op(lambda: nc.vector.tensor_tensor(out=dst[:, lo:], in0=src[:, lo:], in1=src[:, lo - step:16 + TPC - step], op=ALU.add),
                                  reads=[dsrc], writes=[ddst])
                        src, dsrc = dst, ddst
                        step *= 2
                    pl, dpl = pooled[c]
                    fw.dve.op(lambda: nc.vector.scalar_tensor_tensor(out=pl[:], in0=src[:, 16:], scalar=1.0 / w, in1=u[:, 16:],
                                                                     op0=ALU.mult, op1=ALU.subtract), reads=[dsrc, du], writes=[dpl])
                    fw.dve.op(lambda: nc.vector.tensor_tensor(out=t16[:], in0=src[:, 16:32], in1=invc[:, g, :], op=ALU.mult),
                              reads=[dsrc, dinv], writes=[dt16])
                    fw.dve.op(lambda: nc.vector.tensor_tensor(out=pl[:, 0:16], in0=t16[:], in1=u[:, 16:32], op=ALU.subtract),
                              reads=[dt16, du], writes=[dpl])
                    sgt, dsg = sgs[c]
                    for t in range(NT):
                        ps, pd = self.proj_fm(wg, dwg, c * P, P, hT, dh, t * TT, TT)
                        fw.act.op(lambda: nc.scalar.activation(out=sgt[:, t * TT:(t + 1) * TT], in_=ps[:, 0:TT], func=AF.Silu),
                                  reads=[pd], writes=[dsg])
                for dc in range(2):
                    j = 2 * g + dc
                    y, dy, cy = yb.next()
                    for t in range(NT):
                        ps, pd, _ = self.mmp.next()
                        for c in range(2):
                            fw.pe.op(lambda: nc.tensor.matmul(ps[:, 0:TT], lhsT=wpool[:, 2 * g + c, dc * P:(dc + 1) * P],
                                                              rhs=pooled[c][0][:, t * TT:(t + 1) * TT], start=(c == 0), stop=(c == 1)),
                                     reads=[dwp, pooled[c][1]], writes=[pd], inc=(c == 1))
                        fw.dve.op(lambda: nc.vector.scalar_tensor_tensor(out=y[:, t * TT:(t + 1) * TT], in0=ps[:, 0:TT], scalar=pscale[:, j:j + 1],
                                                                         in1=sgs[dc][0][:, t * TT:(t + 1) * TT], op0=ALU.mult, op1=ALU.mult),
                                  reads=[pd, dpsc, sgs[dc][1]], writes=[dy])
                    self.dma(self.ysc[j, :, :], y[:], reads=[dy], writes=[self.d_ysc[j]], ctr=cy)

    def mem_attn(self):
        cfg, nc, fw, io = self.cfg, self.nc, self.fw, self.io
        P, KC, TPC, TT, NT = cfg.P, cfg.KC, cfg.TPC, cfg.TT, cfg.NT
        hT, dh = self.hT, self.d_hT
        with self.scope():
            self.mkw("P4")
            gmem, dgm = self.const("gmem", [P, KC], F32, io["g_mem"][:, :])
            memT = self.sbuf("memT", [P, KC, 2 * P], BF16); dmT = Dep()
            self.norm_T(io["mem"], 2, gmem, dgm, memT, dmT)
            kmT = self.sbuf("kmT", [P, 4, 2 * P], BF16); dkm = Dep()
            vm = self.sbuf("vm", [P, 2, 4 * P], BF16); dvm = Dep()
            w, dw = self.loadw(self.w4P, io["w_mem_kv"][:, 0:4 * P], 4 * P)
            for h in range(4):
                ps, pd = self.proj_fm(w, dw, h * P, P, memT, dmT, 0, 2 * P)
                self.evac(kmT[:, h, :], ps[:, 0:2 * P], [pd], [dkm])
            w, dw = self.loadw(self.w4P, io["w_mem_kv"][:, 4 * P:8 * P], 4 * P)
            for mb in range(2):
                ps, pd, _ = self.mmp.next()
                self.mm_chain(ps, pd, P, 4 * P, lambda c: memT[:, c, mb * P:(mb + 1) * P], lambda c: w[:, c, 0:4 * P], KC, [dmT, dw])
                self.evac(vm[:, mb, :], ps[:, 0:4 * P], [pd], [dvm])
            qT = Rot(self, "mq", [P, TT], BF16, 2, dma=False)
            Er = Rot(self, "mE", [P, TT], BF16, 4, dma=False)
            rd = Rot(self, "mrd", [P, TT], F32, 2, dma=False)
            of = Rot(self, "mof", [P, TT], F32, 2, dma=False)
            sgr = Rot(self, "msg", [P, TT], BF16, 2, dma=False)
            yb = Rot(self, "my", [P, TPC], BF16, 2)
            for h in range(4):
                wq, dwq = self.loadw(self.wP, self.win(cfg.o_mq + h * P, P), P)
                wg, dwg = self.loadw(self.wP, self.win(cfg.o_mgate + h * P, P), P)
                y, dy, cy = yb.next()
                for t in range(NT):
                    ps, pd = self.proj_fm(wq, dwq, 0, P, hT, dh, t * TT, TT)
                    q, dq, _ = qT.next()
                    self.evac(q[:], ps[:, 0:TT], [pd], [dq])
                    Es = []
                    for mb in range(2):
                        sp_, dsp, _ = self.auxp.next()
                        fw.pe.op(lambda: nc.tensor.matmul(sp_[:, 0:TT], lhsT=kmT[:, h, mb * P:(mb + 1) * P], rhs=q[:], start=True, stop=True),
                                 reads=[dkm, dq], writes=[dsp])
                        E, dE, _ = Er.next()
                        fw.act.op(lambda: nc.scalar.activation(out=E[:], in_=sp_[:, 0:TT], func=AF.Exp, scale=float(P) ** -0.5),
                                  reads=[dsp], writes=[dE])
                        Es.append((E, dE))
                    op_, dop, _ = self.auxp.next()
                    dp_, ddp, _ = self.auxp.next()
                    for mb in range(2):
                        fw.pe.op(lambda: nc.tensor.matmul(op_[:, 0:TT], lhsT=vm[:, mb, h * P:(h + 1) * P], rhs=Es[mb][0][:], start=(mb == 0), stop=(mb == 1)),
                                 reads=[dvm, Es[mb][1]], writes=[dop])
                    for mb in range(2):
                        fw.pe.op(lambda: nc.tensor.matmul(dp_[:, 0:TT], lhsT=self.ones[:], rhs=Es[mb][0][:], start=(mb == 0), stop=(mb == 1)),
                                 reads=[self.d_ones, Es[mb][1]], writes=[ddp])
                    r_, dr_, _ = rd.next()
                    fw.dve.op(lambda: nc.vector.reciprocal(out=r_[:], in_=dp_[:, 0:TT]), reads=[ddp], writes=[dr_])
                    o_, do_, _ = of.next()
                    fw.dve.op(lambda: nc.vector.tensor_tensor(out=o_[:], in0=op_[:, 0:TT], in1=r_[:], op=ALU.mult), reads=[dop, dr_], writes=[do_])
                    ps, pd = self.proj_fm(wg, dwg, 0, P, hT, dh, t * TT, TT)
                    s_, ds_, _ = sgr.next()
                    fw.act.op(lambda: nc.scalar.activation(out=s_[:], in_=ps[:, 0:TT], func=AF.Silu), reads=[pd], writes=[ds_])
                    fw.dve.op(lambda: nc.vector.tensor_tensor(out=y[:, t * TT:(t + 1) * TT], in0=o_[:], in1=s_[:], op=ALU.mult),
                              reads=[do_, ds_], writes=[dy])
                self.dma(self.ysc[20 + h, :, :], y[:], reads=[dy], writes=[self.d_ysc[20 + h]], ctr=cy)

    def swa(self):
        cfg, nc, fw, io = self.cfg, self.nc, self.fw, self.io
        P, KC, TPC, TT, NT = cfg.P, cfg.KC, cfg.TPC, cfg.TT, cfg.NT
        hT, dh = self.hT, self.d_hT
        with self.scope():
            self.mkw("P")
            flags, dfl = self.const("flags", [P, 2], F32, io["flags"][:, :])
            zer = self.sbuf("zer", [P, 1], F32); dz = Dep()
            fw.dve.op(lambda: nc.vector.memset(zer[:], 0.0), writes=[dz])
            num = self.sbuf("snum", [P, TPC], F32); dnum = Dep()
            den = self.sbuf("sden", [P, TPC], F32); dden = Dep()
            QT = self.sbuf("sQT", [P, TPC], BF16); dQT = Dep()
            KTo = Rot(self, "sKo", [P, TPC], BF16, 2)
            KTh = Rot(self, "sKh", [P, TPC], BF16, 2)
            Vo = Rot(self, "sVo", [P, 16, P], BF16, 2)
            Vh = Rot(self, "sVh", [P, 16, P], BF16, 2)
            bi = Rot(self, "sbi", [P, 2, P], F32, 2)
            tmpr = Rot(self, "stmp", [P, P], F32, 3, dma=False)
            Er = Rot(self, "sE", [P, P], BF16, 4, dma=False)
            sg = self.sbuf("ssg", [P, TPC], BF16); dsg = Dep()
            yb = Rot(self, "sy", [P, TPC], BF16, 2)
            sc = float(P) ** -0.5
            for s in range(4):
                for g in range(3):
                    head = 4 * g + s; dil = 4 ** g; nblk = 16 // dil
                    wq, dwq = self.loadw(self.wP, self.win(cfg.o_sq + head * P, P), P)
                    for t in range(NT):
                        ps, pd = self.proj_fm(wq, dwq, 0, P, hT, dh, t * TT, TT)
                        self.evac(QT[:, t * TT:(t + 1) * TT], ps[:, 0:TT], [pd], [dQT])
                    ko, dko, cko = KTo.next(); kh, dkh, ckh = KTh.next()
                    vo, dvo, cvo = Vo.next(); vh, dvh, cvh = Vh.next()
                    b2, db2, cb2 = bi.next()
                    self.dma(ko[:], io["kt_own"][head, :, :], writes=[dko], ctr=cko)
                    self.dma(kh[:, 0:dil * P], io["kt_halo"][head, :, TPC - dil * P:TPC], writes=[dkh], ctr=ckh)
                    self.dma(vo[:], io["v_own"][g, :, :, s * P:(s + 1) * P].rearrange("s k d -> k s d"), writes=[dvo], ctr=cvo)
                    self.dma(vh[:, 0:dil, :], io["v_halo"][g, 16 - dil:16, :, s * P:(s + 1) * P].rearrange("s k d -> k s d"), writes=[dvh], ctr=cvh)
                    self.dma(b2[:], io["c_bias"][head, :, :, :], writes=[db2], ctr=cb2)
                    for n in range(nblk):
                        for r in range(dil):
                            qs = slice(n * dil * P + r, (n + 1) * dil * P, dil)
                            np_, dnp, _ = self.auxp.next()
                            dp_, ddp, _ = self.auxp.next()
                            for kb in range(2):
                                if kb == 1:
                                    k_ap, dk = ko[:, qs], dko
                                    v_ap, dv = vo[:, n * dil + r, :], dvo
                                    nf = zer[:, 0:1]
                                elif n > 0:
                                    k_ap, dk = ko[:, slice((n - 1) * dil * P + r, n * dil * P, dil)], dko
                                    v_ap, dv = vo[:, (n - 1) * dil + r, :], dvo
                                    nf = zer[:, 0:1]
                                else:
                                    k_ap, dk = kh[:, slice(r, dil * P, dil)], dkh
                                    v_ap, dv = vh[:, r, :], dvh
                                    nf = flags[:, 0:1]
                                sp_, dsp, _ = self.auxp.next()
                                fw.pe.op(lambda: nc.tensor.matmul(sp_[:, 0:P], lhsT=k_ap, rhs=QT[:, qs], start=True, stop=True),
                                         reads=[dk, dQT], writes=[dsp])
                                tm, dtm, _ = tmpr.next()
                                fw.dve.op(lambda: nc.vector.scalar_tensor_tensor(out=tm[:], in0=sp_[:, 0:P], scalar=sc, in1=b2[:, kb, :],
                                                                                 op0=ALU.mult, op1=ALU.add), reads=[dsp, db2], writes=[dtm])
                                E, dE, _ = Er.next()
                                fw.act.op(lambda: nc.scalar.activation(out=E[:], in_=tm[:], func=AF.Exp, bias=nf), reads=[dtm, dfl, dz], writes=[dE])
                                fw.pe.op(lambda: nc.tensor.matmul(np_[:, 0:P], lhsT=v_ap, rhs=E[:], start=(kb == 0), stop=(kb == 1)),
                                         reads=[dv, dE], writes=[dnp])
                                fw.pe.op(lambda: nc.tensor.matmul(dp_[:, 0:P], lhsT=self.ones[:], rhs=E[:], start=(kb == 0), stop=(kb == 1)),
                                         reads=[self.d_ones, dE], writes=[ddp])
                            if g == 0:
                                fw.act.op(lambda: nc.scalar.copy(out=num[:, qs], in_=np_[:, 0:P]), reads=[dnp], writes=[dnum])
                                fw.dve.op(lambda: nc.vector.tensor_copy(out=den[:, qs], in_=dp_[:, 0:P]), reads=[ddp], writes=[dden])
                            else:
                                fw.dve.op(lambda: nc.vector.tensor_tensor(out=num[:, qs], in0=np_[:, 0:P], in1=num[:, qs], op=ALU.add),
                                          reads=[dnp, dnum], writes=[dnum])
                                fw.dve.op(lambda: nc.vector.tensor_tensor(out=den[:, qs], in0=dp_[:, 0:P], in1=den[:, qs], op=ALU.add),
                                          reads=[ddp, dden], writes=[dden])
                fw.dve.op(lambda: nc.vector.reciprocal(out=den[:], in_=den[:]), reads=[dden], writes=[dden])
                fw.dve.op(lambda: nc.vector.tensor_tensor(out=num[:], in0=num[:], in1=den[:], op=ALU.mult), reads=[dnum, dden], writes=[dnum])
                wg, dwg = self.loadw(self.wP, self.win(cfg.o_sgate + s * P, P), P)
                for t in range(NT):
                    ps, pd = self.proj_fm(wg, dwg, 0, P, hT, dh, t * TT, TT)
                    fw.act.op(lambda: nc.scalar.activation(out=sg[:, t * TT:(t + 1) * TT], in_=ps[:, 0:TT], func=AF.Silu), reads=[pd], writes=[dsg])
                y, dy, cy = yb.next()
                fw.dve.op(lambda: nc.vector.tensor_tensor(out=y[:], in0=num[:], in1=sg[:], op=ALU.mult), reads=[dnum, dsg], writes=[dy])
                self.dma(self.ysc[8 + s, :, :], y[:], reads=[dy], writes=[self.d_ysc[8 + s]], ctr=cy)

    def merge_out(self):
        cfg, nc, fw, io = self.cfg, self.nc, self.fw, self.io
        P, KC, TPC, TT, NT, D = cfg.P, cfg.KC, cfg.TPC, cfg.TT, cfg.NT, cfg.D
        hT, dh = self.hT, self.d_hT
        br_off = (0, 8, 12, 20); br_kc = (8, 4, 8, 4)
        br_w = ("w_br_pool", "w_br_swa", "w_br_gla", "w_br_mem")
        with self.scope():
            self.mkw("P")
            bm, dbm = self.const("bmerge", [P, 4, KC], F32, io["b_merge"][:, :, :])
            gpost, dgp = self.const("gpost", [P, D], F32, io["g_post_bc"][:, :])
            mT = self.sbuf("mT", [P, KC, TT], BF16); dmT = Dep()
            c_out = self.dctr(); self.out_ctrs.append(c_out)
            for t in range(NT):
                with self.scope():
                    yT = self.sbuf("yT", [P, cfg.NY, TT], BF16); dyT = Dep(); cyT = self.dctr()
                    self.dma(yT[:], self.ysc[:, :, t * TT:(t + 1) * TT].rearrange("j p t -> p j t"), reads=self.d_ysc, writes=[dyT], ctr=cyT)
                    wbr = Rot(self, "wbr", [P, 8, P], BF16, 4, sw=True)
                    sig = Rot(self, "sig", [P, TT], F32, 2, dma=False)
                    acc = self.sbuf("macc", [P, TT], F32); dacc = Dep()
                    tmp = self.sbuf("mtmp", [P, TT], F32); dtmp = Dep()
                    for m in range(KC):
                        for b in range(4):
                            wgt, dwg = self.loadw(self.wP, self.win(cfg.o_gl + b * D + m * P, P), P)
                            wb, dwb = self.loadw(wbr, io[br_w[b]][:, m * P:(m + 1) * P], P)
                            gp, dgp_ = self.proj_fm(wgt, dwg, 0, P, hT, dh, t * TT, TT)
                            sg, dsg, _ = sig.next()
                            fw.act.op(lambda: nc.scalar.activation(out=sg[:], in_=gp[:, 0:TT], func=AF.Sigmoid, bias=bm[:, b, m:m + 1]),
                                      reads=[dgp_, dbm], writes=[dsg])
                            zp, dzp, _ = self.mmp.next()
                            self.mm_chain(zp, dzp, P, TT, lambda c: wb[:, c, 0:P], lambda c: yT[:, br_off[b] + c, :], br_kc[b], [dwb, dyT])
                            if b == 0:
                                fw.dve.op(lambda: nc.vector.tensor_tensor(out=acc[:], in0=zp[:, 0:TT], in1=sg[:], op=ALU.mult),
                                          reads=[dzp, dsg], writes=[dacc])
                            else:
                                fw.dve.op(lambda: nc.vector.tensor_tensor(out=tmp[:], in0=zp[:, 0:TT], in1=sg[:], op=ALU.mult),
                                          reads=[dzp, dsg], writes=[dtmp])
                                o = mT[:, m, :] if b == 3 else acc[:]
                                fw.dve.op(lambda: nc.vector.tensor_tensor(out=o, in0=acc[:], in1=tmp[:], op=ALU.add),
                                          reads=[dacc, dtmp], writes=[dmT if b == 3 else dacc])
                with self.scope():
                    self.mkw("2")
                    osb = self.sbuf("osb", [P, 4, D], F32); dosb = Dep()
                    for ob in range(D // (2 * P)):
                        w, dw = self.loadw(self.w2P, io["w_out"][:, ob * 2 * P:(ob + 1) * 2 * P], 2 * P)
                        for tb in range(4):
                            ps, pd, _ = self.mmp.next()
                            self.mm_chain(ps, pd, P, 2 * P, lambda c: mT[:, c, tb * P:(tb + 1) * P], lambda c: w[:, c, 0:2 * P], KC, [dmT, dw])
                            self.evac(osb[:, tb, ob * 2 * P:(ob + 1) * 2 * P], ps[:, 0:2 * P], [pd], [dosb])
                    xr = Rot(self, "xres", [P, D], F32, 2)
                    junk = self.sbuf("ojunk", [P, D], BF16); dj = Dep()
                    ssr = Rot(self, "oss", [P, 2], F32, 2, dma=False)
                    for tb in range(4):
                        row0 = t * TT + tb * P
                        xt, dx, cx = xr.next()
                        self.dma(xt[:], io["x"][row0:row0 + P, :], writes=[dx], ctr=cx)
                        ss, dss, _ = ssr.next()
                        fw.act.op(lambda: nc.scalar.activation(out=junk[:], in_=osb[:, tb, :], func=AF.Square, accum_out=ss[:, 0:1]),
                                  reads=[dosb], writes=[dj, dss])
                        fw.act.op(lambda: nc.scalar.activation(out=ss[:, 1:2], in_=ss[:, 0:1], func=AF.Ln, scale=1.0 / D, bias=cfg.EPS),
                                  reads=[dss], writes=[dss])
                        fw.act.op(lambda: nc.scalar.activation(out=ss[:, 0:1], in_=ss[:, 1:2], func=AF.Exp, scale=-0.5), reads=[dss], writes=[dss])
                        fw.dve.op(lambda: nc.vector.scalar_tensor_tensor(out=osb[:, tb, :], in0=osb[:, tb, :], scalar=ss[:, 0:1], in1=gpost[:],
                                                                         op0=ALU.mult, op1=ALU.mult), reads=[dosb, dss, dgp], writes=[dosb])
                        fw.dve.op(lambda: nc.vector.tensor_tensor(out=xt[:], in0=xt[:], in1=osb[:, tb, :], op=ALU.add), reads=[dx, dosb], writes=[dx])
                        self.dma(io["out"][row0:row0 + P, :], xt[:], reads=[dx], ctr=c_out)


def build_program(cfg, phase):
    P, KC, D, TPC = cfg.P, cfg.KC, cfg.D, cfg.TPC
    nc = bass.Bass("TRN2", target_bir_lowering=False)
    io = {}

    def inp(name, shape, dt=F32):
        io[name] = nc.dram_tensor(name, list(shape), dt, kind="ExternalInput").ap()

    def outp(name, shape, dt=F32):
        io[name] = nc.dram_tensor(name, list(shape), dt, kind="ExternalOutput").ap()

    inp("x", [TPC, D]); inp("w_in", [D, cfg.DIN]); inp("g_pre", [P, KC])
    inp("c_ident", [P, P]); inp("c_ones", [P, P]); inp("c_U", [P, P]); inp("c_Mrev", [P, P])
    inp("w_alpha_aug", [32, 4 * P])
    if phase == "A":
        outp("o_kt", [12, P, TPC], BF16); outp("o_v", [3, 16, P, 4 * P], BF16)
        outp("o_tail", [P, 8, 16]); outp("o_glaS", [P, 4, 2 * P]); outp("o_glaD", [P, 4])
    else:
        inp("c_Mc", [P, P]); inp("c_bias", [12, P, 2, P])
        inp("g_gla", [P, 8]); inp("mem", [2 * P, D]); inp("g_mem", [P, KC]); inp("w_mem_kv", [D, 8 * P])
        inp("w_pool", [8 * P, 2 * P]); inp("pool_scale", [P, 8]); inp("invcnt", [P, 4, 16])
        inp("w_br_pool", [8 * P, D]); inp("w_br_swa", [4 * P, D]); inp("w_br_gla", [8 * P, D]); inp("w_br_mem", [4 * P, D])
        inp("w_out", [D, D]); inp("b_merge", [P, 4, KC]); inp("g_post_bc", [P, D])
        inp("kt_own", [12, P, TPC], BF16); inp("kt_halo", [12, P, TPC], BF16)
        inp("v_own", [3, 16, P, 4 * P], BF16); inp("v_halo", [3, 16, P, 4 * P], BF16)
        inp("pool_halo", [P, 8, 16]); inp("S_all", [8, P, 4, 2 * P]); inp("D_all", [8, P, 4])
        inp("sel", [P, 8]); inp("flags", [P, 2])
        outp("out", [TPC, D])
        io["ysc"] = nc.dram_tensor("ysc", [cfg.NY, P, TPC], BF16).ap()
    K = Kern(nc, cfg, phase)
    K.setup(io)
    if phase == "A":
        K.phaseA()
    else:
        K.phaseB()
    K.barrier()
    K.fw.finish(K.fw.all_dctrs)
    K.stack[0].close()
    return nc

def host_consts(cfg):
    P = cfg.P
    f = np.float32
    j = np.arange(P)[:, None]; i = np.arange(P)[None, :]
    c = {}
    c["c_ident"] = np.eye(P, dtype=f)
    c["c_ones"] = np.ones((P, P), f)
    c["c_U"] = np.where(j <= i, -1.0 / 16.0, 0.0).astype(f)
    c["c_Mrev"] = np.where(j > i, -1.0 / 16.0, 0.0).astype(f)
    c["c_Mc"] = np.where(j <= i, 1.0, 0.0).astype(f)
    slopes = np.exp2(-8.0 * (np.arange(12, dtype=np.float64) + 1.0) / 12.0)
    bias = np.zeros((12, P, 2, P), f)
    k = np.arange(P)[:, None]; q = np.arange(P)[None, :]
    for h in range(12):
        dil = 4 ** (h // 4)
        dprev = (q - k + P).astype(np.float64)
        dcur = (q - k).astype(np.float64)
        bias[h, :, 0, :] = np.where(k >= q, -slopes[h] * dprev * dil, -30000.0)
        bias[h, :, 1, :] = np.where(k <= q, -slopes[h] * dcur * dil, -30000.0)
    c["c_bias"] = bias
    return c


def layer_params(cfg, inp, l):
    P, KC, D = cfg.P, cfg.KC, cfg.D
    f = np.float32
    A = lambda a: np.ascontiguousarray(a, dtype=f)
    d = {}
    d["w_in"] = A(inp["w_in"][l])
    d["g_pre"] = A(inp["g_pre"][l].reshape(KC, P).T)
    wa = np.zeros((32, 4 * P), f)
    wa[0:16] = inp["w_alpha"][l]; wa[16] = inp["b_alpha"][l]
    d["w_alpha_aug"] = wa
    dB = {}
    dB["g_gla"] = A(inp["g_gla"][l].reshape(8, P).T)
    dB["mem"] = A(inp["mem"][0])
    dB["g_mem"] = A(inp["g_mem"][l].reshape(KC, P).T)
    dB["w_mem_kv"] = A(inp["w_mem_kv"][l])
    dB["w_pool"] = A(inp["w_pool"][l].reshape(8 * P, 2 * P))
    dB["pool_scale"] = A(inp["pool_scale"][l].reshape(8, P).T)
    for n in ("w_br_pool", "w_br_swa", "w_br_gla", "w_br_mem", "w_out"):
        dB[n] = A(inp[n][l])
    dB["b_merge"] = A(inp["b_merge"][l].reshape(4, KC, P).transpose(2, 0, 1))
    dB["g_post_bc"] = A(np.broadcast_to(inp["g_post"][l][None, :], (P, D)))
    return d, dB


def core_consts(cfg, c):
    P = cfg.P
    f = np.float32
    sel = np.zeros((P, 8), f); sel[:, c] = 1.0
    flags = np.zeros((P, 2), f)
    if c == 0:
        flags[:, 0] = -30000.0
    inv = np.zeros((P, 4, 16), f)
    for g, w in enumerate((2, 4, 8, 16)):
        if c == 0:
            inv[:, g, :] = 1.0 / np.minimum(np.arange(16) + 1, w)
        else:
            inv[:, g, :] = 1.0 / w
    return {"sel": sel, "flags": flags, "invcnt": inv}


def run_layer(cfg, ncA, ncB, runner, consts, inp, l, xs):
    n = len(xs)
    pA, pB = layer_params(cfg, inp, l)
    cA = {k: consts[k] for k in ("c_ident", "c_ones", "c_U", "c_Mrev")}
    mapsA = [dict(pA, **cA, x=xs[c]) for c in range(n)]
    rA = runner(ncA, mapsA)
    S_all = np.zeros((8,) + rA[0]["o_glaS"].shape, np.float32)
    D_all = np.ones((8,) + rA[0]["o_glaD"].shape, np.float32)
    for c in range(n):
        S_all[c] = rA[c]["o_glaS"]; D_all[c] = rA[c]["o_glaD"]
    mapsB = []
    for c in range(n):
        m = dict(pA, **pB, **consts, **core_consts(cfg, c), x=xs[c])
        m["kt_own"] = rA[c]["o_kt"]; m["v_own"] = rA[c]["o_v"]
        if c > 0:
            m["kt_halo"] = rA[c - 1]["o_kt"]; m["v_halo"] = rA[c - 1]["o_v"]; m["pool_halo"] = rA[c - 1]["o_tail"]
        else:
            m["kt_halo"] = np.zeros_like(rA[0]["o_kt"]); m["v_halo"] = np.zeros_like(rA[0]["o_v"])
            m["pool_halo"] = np.zeros_like(rA[0]["o_tail"])
        m["S_all"] = S_all; m["D_all"] = D_all
        mapsB.append(m)
    rB = runner(ncB, mapsB)
    return [rB[c]["out"] for c in range(n)]


_PROGS = {}


def _runner(nc, maps):
    return run_bass_kernel_spmd(nc, maps, core_ids=list(range(len(maps)))).results


def kernel(**inp):
    cfg = Cfg(128, 16)
    inp = {k: np.asarray(v) for k, v in inp.items()}
    if "A" not in _PROGS:
        _PROGS["A"] = build_program(cfg, "A")
        _PROGS["B"] = build_program(cfg, "B")
    consts = host_consts(cfg)
    x = inp["x"]
    xs = [np.ascontiguousarray(x[0, c * cfg.TPC:(c + 1) * cfg.TPC], dtype=np.float32) for c in range(cfg.NC)]
    depth = inp["w_in"].shape[0]
    for l in range(depth):
        xs = run_layer(cfg, _PROGS["A"], _PROGS["B"], _runner, consts, inp, l, xs)
    return np.concatenate(xs, 0)[None].astype(np.float32)
```
